# Optimizing a Trainium2 kernel written in Bass

```python
import jax, jax.numpy as jnp
from jax import lax
import numpy as np

D_MODEL = 2048
BATCH = 1
SEQ = 8192
DEPTH = 4

CHUNK = 64
D_SSM = D_MODEL
SSM_HEAD_DIM = 64
SSM_HEADS = D_SSM // SSM_HEAD_DIM
SSM_GROUPS = 8
HEADS_PER_GROUP = SSM_HEADS // SSM_GROUPS
D_STATE = 128
SSM_CONV = 4
D_XBC = D_SSM + 2 * SSM_GROUPS * D_STATE
NORM_GROUPS = 8
D_CONV = D_MODEL
CONV_GROUPS = 32
CONFORMER_KERNEL = 31
D_MIX = D_SSM + D_CONV
SPLITS = [D_SSM, D_SSM + D_XBC, D_SSM + D_XBC + SSM_HEADS,
          D_SSM + D_XBC + SSM_HEADS + D_CONV, D_SSM + D_XBC + SSM_HEADS + 2 * D_CONV]
D_IN = D_SSM + D_XBC + SSM_HEADS + 3 * D_CONV
EPS = 1e-5

kernel_name = "hybrid_ssd_conformer_conv_trunk"


def rmsnorm(x, w):
    xf = x.astype(jnp.float32)
    y = xf * lax.rsqrt(jnp.mean(xf * xf, axis=-1, keepdims=True) + EPS)
    return (y * w.astype(jnp.float32)).astype(x.dtype)


def causal_depthwise_conv(x, w, b):
    k, c = w.shape
    y = lax.conv_general_dilated(x, w[:, None, :].astype(x.dtype), window_strides=(1,),
                                 padding=[(k - 1, 0)], dimension_numbers=('NWC', 'WIO', 'NWC'),
                                 feature_group_count=c)
    return y + b.astype(x.dtype)


def ssd_chunked(x, dt, a, bmat, cmat):
    b, l, h, p = x.shape
    g, n = bmat.shape[-2:]
    r = h // g
    c = l // CHUNK
    q = CHUNK
    xf = x.astype(jnp.float32)
    xdt = (xf * dt[..., None]).reshape(b, c, q, g, r, p)
    adt = jnp.moveaxis((dt * a).reshape(b, c, q, g, r), 2, -1)
    a_cs = jnp.cumsum(adt, axis=-1)
    bc = bmat.astype(jnp.float32).reshape(b, c, q, g, n)
    cc = cmat.astype(jnp.float32).reshape(b, c, q, g, n)
    causal = jnp.tril(jnp.ones((q, q), dtype=bool))
    seg = a_cs[..., :, None] - a_cs[..., None, :]
    decay = jnp.exp(jnp.where(causal, seg, -jnp.inf))
    cb = jnp.einsum('bclgn,bcsgn->bcgls', cc, bc)
    y_diag = jnp.einsum('bcgls,bcgrls,bcsgrp->bclgrp', cb, decay, xdt)
    decay_states = jnp.exp(a_cs[..., -1:] - a_cs)
    states = jnp.einsum('bcsgn,bcgrs,bcsgrp->bcgrpn', bc, decay_states, xdt)
    chunk_decay = jnp.exp(a_cs[..., -1])

    def step(carry, inp):
        st, dec = inp
        return carry * dec[..., None, None] + st, carry

    init = jnp.zeros((b, g, r, p, n), jnp.float32)
    _, prev = lax.scan(step, init, (jnp.moveaxis(states, 1, 0), jnp.moveaxis(chunk_decay, 1, 0)))
    prev = jnp.moveaxis(prev, 0, 1)
    y_off = jnp.einsum('bclgn,bcgrpn,bcgrl->bclgrp', cc, prev, jnp.exp(a_cs))
    return (y_diag + y_off).reshape(b, l, h, p)


def hybrid_layer(h, norm_w, w_in, ssm_conv_w, ssm_conv_b, dt_bias, a_log, d_skip, ssm_norm_w,
                 conf_conv_w, conf_conv_b, ln_w, ln_b, w_out):
    b, l, _ = h.shape
    u = rmsnorm(h, norm_w)
    proj = jnp.einsum('bld,de->ble', u, w_in.astype(u.dtype))
    z, xbc, dt_raw, cv, ca, cg = jnp.split(proj, SPLITS, axis=-1)

    xbc = jax.nn.silu(causal_depthwise_conv(xbc, ssm_conv_w, ssm_conv_b))
    xs, bm, cm = jnp.split(xbc, [D_SSM, D_SSM + SSM_GROUPS * D_STATE], axis=-1)
    dt = jax.nn.softplus(dt_raw.astype(jnp.float32) + dt_bias.astype(jnp.float32))
    a = -jnp.exp(a_log.astype(jnp.float32))
    xh = xs.reshape(b, l, SSM_HEADS, SSM_HEAD_DIM)
    y = ssd_chunked(xh, dt, a, bm.reshape(b, l, SSM_GROUPS, D_STATE),
                    cm.reshape(b, l, SSM_GROUPS, D_STATE))
    y = y + d_skip.astype(jnp.float32)[:, None] * xh.astype(jnp.float32)
    yz = (y.reshape(b, l, D_SSM) * jax.nn.silu(z.astype(jnp.float32))).reshape(b, l, NORM_GROUPS, -1)
    yz = yz * lax.rsqrt(jnp.mean(yz * yz, axis=-1, keepdims=True) + EPS)
    y_ssd = yz.reshape(b, l, D_SSM) * ssm_norm_w.astype(jnp.float32)

    glu = cv * jax.nn.sigmoid(ca)
    cc = causal_depthwise_conv(glu, conf_conv_w, conf_conv_b).astype(jnp.float32)
    mu = jnp.mean(cc, axis=-1, keepdims=True)
    var = jnp.mean(jnp.square(cc - mu), axis=-1, keepdims=True)
    cc = (cc - mu) * lax.rsqrt(var + EPS) * ln_w.astype(jnp.float32) + ln_b.astype(jnp.float32)
    y_conv = jax.nn.silu(cc) * jax.nn.silu(cg.astype(jnp.float32))

    mixed = jnp.concatenate([y_ssd, y_conv], axis=-1).astype(h.dtype)
    return h + jnp.einsum('ble,ed->bld', mixed, w_out.astype(h.dtype))


def setup_inputs(seed: int = 0) -> dict:
    key = jax.random.key(seed)
    ks = jax.random.split(key, 16)
    f32 = jnp.float32
    x = jax.random.normal(ks[0], (BATCH, SEQ, D_MODEL), f32)
    norm_w = 1.0 + 0.02 * jax.random.normal(ks[1], (DEPTH, D_MODEL), f32)
    w_in = jax.random.normal(ks[2], (DEPTH, D_MODEL, D_IN), f32) * D_MODEL ** -0.5
    ssm_conv_w = jax.random.normal(ks[3], (DEPTH, SSM_CONV, D_XBC), f32) * SSM_CONV ** -0.5
    ssm_conv_b = 0.02 * jax.random.normal(ks[4], (DEPTH, D_XBC), f32)
    dt0 = jnp.exp(jax.random.uniform(ks[5], (DEPTH, SSM_HEADS), f32,
                                     minval=float(np.log(1e-3)), maxval=float(np.log(1e-1))))
    dt_bias = dt0 + jnp.log(-jnp.expm1(-dt0))
    a_log = jnp.log(jax.random.uniform(ks[6], (DEPTH, SSM_HEADS), f32, minval=1.0, maxval=16.0))
    d_skip = 1.0 + 0.02 * jax.random.normal(ks[7], (DEPTH, SSM_HEADS), f32)
    ssm_norm_w = 1.0 + 0.02 * jax.random.normal(ks[8], (DEPTH, D_SSM), f32)
    conf_conv_w = jax.random.normal(ks[9], (DEPTH, CONFORMER_KERNEL, D_CONV), f32) * CONFORMER_KERNEL ** -0.5
    conf_conv_b = 0.02 * jax.random.normal(ks[10], (DEPTH, D_CONV), f32)
    ln_w = 1.0 + 0.02 * jax.random.normal(ks[11], (DEPTH, D_CONV), f32)
    ln_b = 0.02 * jax.random.normal(ks[12], (DEPTH, D_CONV), f32)
    w_out = jax.random.normal(ks[13], (DEPTH, D_MIX, D_MODEL), f32) * D_MIX ** -0.5
    final_norm_w = 1.0 + 0.02 * jax.random.normal(ks[14], (D_MODEL,), f32)
    return {"x": x, "norm_w": norm_w, "w_in": w_in, "ssm_conv_w": ssm_conv_w,
            "ssm_conv_b": ssm_conv_b, "dt_bias": dt_bias, "a_log": a_log, "d_skip": d_skip,
            "ssm_norm_w": ssm_norm_w, "conf_conv_w": conf_conv_w, "conf_conv_b": conf_conv_b,
            "ln_w": ln_w, "ln_b": ln_b, "w_out": w_out, "final_norm_w": final_norm_w}


def reference(x, norm_w, w_in, ssm_conv_w, ssm_conv_b, dt_bias, a_log, d_skip, ssm_norm_w,
              conf_conv_w, conf_conv_b, ln_w, ln_b, w_out, final_norm_w):
    h = x
    for i in range(DEPTH):
        h = hybrid_layer(h, norm_w[i], w_in[i], ssm_conv_w[i], ssm_conv_b[i], dt_bias[i], a_log[i],
                         d_skip[i], ssm_norm_w[i], conf_conv_w[i], conf_conv_b[i], ln_w[i], ln_b[i],
                         w_out[i])
    return rmsnorm(h, final_norm_w)
```

```python
import contextlib
import numpy as np
import concourse.bass as bass
import concourse.mybir as mybir
from concourse.bass_utils import run_bass_kernel_spmd

F32 = mybir.dt.float32
BF16 = mybir.dt.bfloat16
AF = mybir.ActivationFunctionType
ALU = mybir.AluOpType

D = 2048
T = 1024
HALO = 32
TE = T + HALO
NT = 8
EPS = 1e-5
NCORES = 8
DEPTH = 4


STOP = None
DEBUG = False


class _Stop(Exception):
    pass


def cut(n):
    if STOP == n:
        raise _Stop()


class _Op:
    __slots__ = ("stream", "fn", "deps", "dma_key", "value", "sem", "has_dep")

    def __init__(self, stream, fn, dma_key):
        self.stream = stream
        self.fn = fn
        self.deps = None
        self.dma_key = dma_key
        self.value = None
        self.sem = None
        self.has_dep = False


class Sched:
    STREAMS = ("pe", "act", "dve", "pool", "sp")
    EPOCH = 30000

    def __init__(self, nc):
        self.nc = nc
        self.ops = {s: [] for s in self.STREAMS}
        self.last_w = {}
        self.readers = {}
        self.final_waits = []

    def op(self, stream, fn, reads=(), writes=(), dma=None):
        deps = set()
        for r in reads:
            w = self.last_w.get(r)
            if w is not None:
                deps.add(w)
        for r in writes:
            w = self.last_w.get(r)
            if w is not None:
                deps.add(w)
            for o in self.readers.get(r, {}).values():
                deps.add(o)
        o = _Op(stream, fn, dma)
        real = []
        for d in deps:
            if stream == "pe" and d.stream == "pe" and d.dma_key is None:
                continue
            d.has_dep = True
            real.append(d)
        o.deps = real
        self.ops[stream].append(o)
        for r in reads:
            self.readers.setdefault(r, {})[(stream, dma)] = o
        for r in writes:
            self.last_w[r] = o
            self.readers[r] = {}
        return o

    def must_finish(self, o):
        o.has_dep = True
        self.final_waits.append(o)

    def emit(self):
        nc = self.nc
        dma_count = {}
        for s in self.STREAMS:
            cnt = 0
            for o in self.ops[s]:
                if o.dma_key is not None:
                    k = ("dma", o.dma_key)
                    c = dma_count.get(k, 0) + (16 if o.dma_key[0] != "cc" else 1)
                    dma_count[k] = c
                    o.sem = k
                    o.value = c
                elif o.has_dep:
                    cnt += 1
                    o.sem = ("eng", s, (cnt - 1) // self.EPOCH)
                    o.value = (cnt - 1) % self.EPOCH + 1
        keys = []
        seen = set()
        for s in self.STREAMS:
            for o in self.ops[s]:
                if o.sem is not None and o.sem not in seen:
                    seen.add(o.sem)
                    keys.append(o.sem)
        with contextlib.ExitStack() as st:
            sems = {}
            for i, k in enumerate(keys):
                sems[k] = st.enter_context(nc.semaphore("s%d" % i))
            block = st.enter_context(nc.Block())
            engs = {"pe": block.tensor, "act": block.scalar, "dve": block.vector,
                    "pool": block.gpsimd, "sp": block.sync}
            for s in self.STREAMS:
                ops = self.ops[s]
                finals = self.final_waits if s == "sp" else []
                if not ops and not finals:
                    continue

                def body(eng, ops=ops, finals=finals):
                    waited = {}
                    for o in ops:
                        for d in o.deps:
                            if waited.get(d.sem, 0) >= d.value:
                                continue
                            eng.wait_ge(sems[d.sem], d.value)
                            waited[d.sem] = d.value
                        ins = o.fn(eng)
                        if o.dma_key is not None:
                            if o.dma_key[0] == "cc":
                                ins.then_inc(sems[o.sem])
                            else:
                                ins.then_inc(sems[o.sem], 16)
                        elif o.has_dep:
                            ins.then_inc(sems[o.sem], 1)
                    for d in finals:
                        if waited.get(d.sem, 0) >= d.value:
                            continue
                        eng.wait_ge(sems[d.sem], d.value)
                        waited[d.sem] = d.value

                engs[s](body)


class _DummySched:
    def op(self, *a, **k):
        return None

    def must_finish(self, o):
        pass


def build_layer(ncores, last, part):
    nc = bass.Bass("TRN2", target_bir_lowering=False)

    def IN(name, shape, dt=F32):
        return nc.dram_tensor(name, shape, dt, kind="ExternalInput").ap()

    if part == "A":
        h_d = IN("h", [128, 16, T])
        hh_d = IN("hh", [128, 16, HALO])
        win_d = IN("win", [48, 128, 16, 256])
        wdt_d = IN("wdt", [128, 16, 32])
    else:
        h_d = IN("hpart", [128, 16, T])
        hh_d = win_d = wdt_d = None
    wout_d = IN("wout", [8, 128, 16, 256])
    normw_d = IN("normw", [128, 16])
    scw_d = IN("scw", [128, 32, 4])
    scb_d = IN("scb", [128, 32])
    dtb_d = IN("dtb", [1, 32])
    alog_d = IN("alog", [1, 32])
    dsk_d = IN("dsk", [128, 16])
    snw_d = IN("snw", [1, 2048])
    ccw_d = IN("ccw", [128, 16, 31])
    ccb_d = IN("ccb", [128, 16])
    lnw_d = IN("lnw", [128, 16])
    lnb_d = IN("lnb", [128, 16])
    fnw_d = IN("fnw", [128, 16])
    msk_d = IN("msk", [128, 16])
    hout_d = nc.dram_tensor("hout", [128, 16, T], F32, kind="ExternalOutput").ap()
    out_d = nc.dram_tensor("out", [128, 16, T], F32, kind="ExternalOutput").ap() if (last and part == "B") else None
    kio = "ExternalOutput" if part == "A" else "ExternalInput"
    yloc_d = nc.dram_tensor("yloc", [8, 128, NT, 256], F32, kind=kio).ap()
    zs_d = nc.dram_tensor("zsd", [8, 128, NT, 256], BF16, kind=kio).ap()
    ct_d = nc.dram_tensor("ctd", [128, 8, T], BF16, kind=kio).ap()
    eg_d = nc.dram_tensor("egd", [128, 256], F32, kind=kio).ap()
    if part == "A":
        fsend_d = nc.dram_tensor("fsend", [128, 2080], F32, kind="ExternalOutput").ap()
        frecv_d = None
    else:
        fsend_d = None
        frecv_d = nc.dram_tensor("frecv", [ncores * 128, 2080], F32, kind="ExternalInput").ap()

    S_holder = []
    with contextlib.ExitStack() as st:
      try:
        def SB(name, shape, dt=F32):
            return st.enter_context(nc.sbuf_tensor("sb_" + name, shape, dt))

        def PS(name, shape, dt=F32):
            return st.enter_context(nc.psum_tensor("ps_" + name, shape, dt))

        S_real = Sched(nc)
        S = S_real
        S_holder.append(S_real)
        uT = SB("uT", [128, 16, TE], BF16)
        Wb = [SB("Wb%d" % i, [128, 16, 256], BF16) for i in range(3)]
        mixT = SB("mixT", [128, 16, T], BF16)
        hbuf = mixT[:].bitcast(F32)
        hhb = SB("hhb", [128, 16, HALO])
        sqb = [SB("sqb%d" % i, [128, 512]) for i in range(2)]
        rs = SB("rs", [128, 512])
        rsh = SB("rsh", [128, HALO])
        nmaskf = rs
        ident = SB("ident", [128, 128], BF16)
        identf = SB("identf", [128, 128])
        onesf = SB("onesf", [128, 128])
        onesb = SB("onesb", [128, 128], BF16)
        trif = SB("trif", [128, 128])
        trib = SB("trib", [128, 128], BF16)
        nmask = SB("nmask", [128, 512], BF16)
        normw = SB("normw", [128, 16])
        scw = SB("scw", [128, 32, 4])
        scb = SB("scb", [128, 32])
        dtb = SB("dtb", [128, 32])
        arep = SB("arep", [128, 32])
        dsk = SB("dsk", [128, 16])
        snw = SB("snw", [128, 256])
        ccw = SB("ccw", [128, 16, 31])
        ccb = SB("ccb", [128, 16])
        lnw = SB("lnw", [128, 16])
        lnb = SB("lnb", [128, 16])
        fnw = SB("fnw", [128, 16])
        msk = SB("msk", [128, 16])
        wdt = SB("wdt", [128, 16, 32], BF16)
        diagD = SB("diagD", [128, 16, 128], BF16)
        CTall = SB("CTall", [128, 8, T], BF16)
        sm = {n: SB("sm_" + n, [128, 256]) for n in
              ("x", "e", "dt", "adt", "hi", "lo", "acs", "nacs", "tot", "cdec", "dst", "eacs")}
        sm["cum"] = sm["x"]
        sm["eglob"] = sm["e"]
        smb = SB("smb", [128, 256], BF16)
        smb2 = SB("smb2", [128, 256], BF16)
        dtot = SB("dtot", [128, 32])
        raw = [SB("raw%d" % i, [128, TE]) for i in range(2)]
        ctmp = [SB("ctmp%d" % i, [128, T]) for i in range(2)]
        xTg = SB("xTg", [128, 2, T], BF16)
        BTg = SB("BTg", [128, T], BF16)
        zsg = SB("zsg", [128, NT, 256], BF16)
        xarena = SB("xarena", [128, 2, NT, 256], BF16)
        xdt = xarena[:, 0]
        xdec = xarena[:, 1]
        Btok = SB("Btok", [128, NT, 128], BF16)
        X1 = [[SB("X1_%d_%d" % (i, k), [128, 512], BF16) for k in range(2)] for i in range(1)]
        dec = [SB("dec%d" % i, [128, 512], BF16) for i in range(2)]
        MT = [SB("MT%d" % i, [128, 512], BF16) for i in range(2)]
        Sloc = SB("Sloc", [128, 256])
        Sbf = SB("Sbf", [128, 256], BF16)
        t1 = [SB("t1_%d" % i, [128, 256]) for i in range(2)]
        ylb = SB("ylb", [128, NT, 256])
        sig = raw
        glu = [SB("glu%d" % i, [128, TE], BF16) for i in range(2)]
        _dgv = ylb[:].rearrange("p a b -> p (a b)").bitcast(BF16)[:, 0:31 * 128].rearrange("p (k c) -> p k c", k=31)
        dg = [_dgv, _dgv]
        sq2 = [Btok[:].rearrange("p j c -> p (j c)")[:, i * 512:(i + 1) * 512] for i in range(2)]
        mub = ctmp[0][:, 0:512]
        rsb = ctmp[0][:, 512:1024]
        nmr = rs[:]
        lt = sqb
        cgs = [SB("cgs%d" % i, [128, 512], BF16) for i in range(2)]
        hb2 = [SB("hb2_%d" % i, [128, 512]) for i in range(2)]
        Fg = xarena[:].rearrange("p a j q -> p (a j q)").bitcast(F32).rearrange("p (r q) -> p r q", r=8)
        Dg = SB("Dg", [128, 8, 32])
        ar = SB("ar", [128, 32])
        Sin = Sloc
        Sinb = Sbf
        zl = zsg
        yz = [ctmp[1][:, i * 256:(i + 1) * 256] for i in range(2)]
        ysq = ctmp[1][:, 512:768]
        ss = [SB("ss%d" % i, [128, 1]) for i in range(2)]
        yn = [SB("yn%d" % i, [128, 256], BF16) for i in range(2)]

        acc = [PS("acc%d" % i, [128, 512]) for i in range(3)]
        halo = PS("halo", [128, 512])
        seg = PS("seg", [128, 512])
        misc = PS("misc", [128, 512])
        yg = PS("yg", [128, 512])
        trp = PS("trp", [128, 8, 128], BF16)

        dmac = [0]

        def dbg(name, ap, key, dt=F32):
            if not DEBUG:
                return
            d = nc.dram_tensor("dbg_" + name, list(ap.shape), dt, kind="ExternalOutput").ap()
            o = S.op("sp", lambda e: e.dma_start(out=d, in_=ap), reads=[key], dma=("st", "dbg_" + name))
            S.must_finish(o)

        def load(dst, src, key, stream="sp"):
            dmac[0] += 1
            return S.op(stream, lambda e: e.dma_start(out=dst, in_=src), writes=[key], dma=("ld", key))

        for (t, d, k) in [(normw, normw_d, "normw"), (scw, scw_d, "scw"), (scb, scb_d, "scb"), (dsk, dsk_d, "dsk"),
                          (ccw, ccw_d, "ccw"), (ccb, ccb_d, "ccb"), (lnw, lnw_d, "lnw"), (lnb, lnb_d, "lnb"),
                          (fnw, fnw_d, "fnw"), (msk, msk_d, "msk")]:
            load(t[:], d, k)
        load(dtb[:], dtb_d.partition_broadcast(128), "dtb")
        load(arep[:], alog_d.partition_broadcast(128), "arep")
        if part == "A":
            S.op("pool", lambda e: e.dma_start(out=wdt[:], in_=wdt_d), writes=["wdt"], dma=("ld", "wdt"))
        S.op("pool", lambda e: e.memset(identf[:], 0.0), writes=["identf"])
        S.op("pool", lambda e: e.affine_select(out=identf[:], in_=identf[:], pattern=[[-1, 128]], compare_op=ALU.not_equal,
                                               fill=1.0, base=0, channel_multiplier=1), reads=["identf"], writes=["identf"])
        S.op("pool", lambda e: e.memset(onesf[:], 1.0), writes=["onesf"])
        S.op("pool", lambda e: e.memset(trif[:], 1.0), writes=["trif"])
        S.op("pool", lambda e: e.affine_select(out=trif[:], in_=trif[:], pattern=[[1, 128]], compare_op=ALU.is_ge,
                                               fill=0.0, base=0, channel_multiplier=-1), reads=["trif"], writes=["trif"])
        S.op("pool", lambda e: e.memset(nmaskf[:], 0.0), writes=["rs"])
        S.op("pool", lambda e: e.affine_select(out=nmaskf[:].rearrange("p (a b) -> p a b", a=4), in_=nmaskf[:].rearrange("p (a b) -> p a b", a=4), pattern=[[0, 4], [1, 128]], compare_op=ALU.is_ge,
                                               fill=-30000.0, base=0, channel_multiplier=-1), reads=["rs"], writes=["rs"])
        S.op("dve", lambda e: e.tensor_copy(out=ident[:], in_=identf[:]), reads=["identf"], writes=["ident"])
        S.op("dve", lambda e: e.tensor_copy(out=onesb[:], in_=onesf[:]), reads=["onesf"], writes=["onesb"])
        S.op("dve", lambda e: e.tensor_copy(out=trib[:], in_=trif[:]), reads=["trif"], writes=["trib"])
        S.op("dve", lambda e: e.tensor_copy(out=nmask[:], in_=nmaskf[:]), reads=["rs"], writes=["nmask"])
        S.op("act", lambda e: e.activation(out=arep[:], in_=arep[:], func=AF.Exp), reads=["arep"], writes=["arep"])
        S.op("dve", lambda e: e.tensor_scalar_mul(out=arep[:], in0=arep[:], scalar1=-1.0), reads=["arep"], writes=["arep"])
        S.op("pool", lambda e: e.tensor_tensor(out=diagD[:], in0=identf[:].unsqueeze(1).broadcast_to([128, 16, 128]),
                                               in1=dsk[:].unsqueeze(2).broadcast_to([128, 16, 128]), op=ALU.mult),
             reads=["identf", "dsk"], writes=["diagD"])

        wq = []
        wstate = {"issued": 0}

        def wensure(k):
            while wstate["issued"] <= min(k + 2, len(wq) - 1):
                i = wstate["issued"]
                slot = i % 3
                src = wq[i]
                if src is not None:
                    S.op("pool", lambda e, slot=slot, src=src: e.dma_start(out=Wb[slot][:], in_=src),
                         writes=[("W", slot)], dma=("ld", "W%d" % slot))
                wstate["issued"] += 1
            return k % 3

        order = []
        for g in range(8):
            order += [g * 3 + 0, g * 3 + 1, g * 3 + 2]
        for i in range(8):
            order += [24 + 2 * i, 24 + 2 * i + 1]
        for i in range(8):
            order += [40 + i]
        for u in order:
            wq.append(win_d[u] if part == "A" else None)
        for u in range(8):
            wq.append(wout_d[u] if part == "A" else None)
        for u in range(8):
            wq.append(wout_d[u] if part == "B" else None)
        wpos = [0]

        def next_w(prefetch=True):
            k = wpos[0]
            wpos[0] += 1
            if not prefetch:
                assert wstate["issued"] > k
                return k % 3
            return wensure(k)

        accr = [0]

        def next_acc():
            i = accr[0] % 3
            accr[0] += 1
            return i

        def rmsnorm_pass(src_d, srch_d, wt, emit_fn):
            blocks = [("h", 0, HALO)] if srch_d is not None else []
            blocks += [(0, 0, 512), (1, 512, 512)]
            for (half, off, n) in blocks:
                if half == "h":
                    load(hhb[:], srch_d, "hhb")
                    src = lambda c: hhb[:, c, :]
                    key = "hhb"
                    pst = halo[:, 480:512]
                    pkey = "halo"
                    rst = rsh
                    rkey = "rsh"
                else:
                    for q in range(4):
                        S.op("sp", lambda e, q=q, off=off: e.dma_start(out=hbuf[:, 4 * q:4 * q + 4, :], in_=src_d[:, 4 * q:4 * q + 4, off:off + 512]),
                             writes=[("hbuf", q)], dma=("ld", "hbuf%d" % q))
                    src = lambda c: hbuf[:, c, :]
                    key = None
                    ai = next_acc()
                    pst = acc[ai][:]
                    pkey = ("acc", ai)
                    rst = rs
                    rkey = "rs"
                for c in range(16):
                    k = key if key else ("hbuf", c // 4)
                    sb = sqb[c % 2]
                    S.op("act", lambda e, c=c, sb=sb, n=n, src=src: e.activation(out=sb[:, 0:n], in_=src(c), func=AF.Square),
                         reads=[k], writes=[("sqb", c % 2)])
                    S.op("pe", lambda e, c=c, sb=sb, n=n, pst=pst: e.matmul(pst, lhsT=onesf[:], rhs=sb[:, 0:n], start=(c == 0), stop=(c == 15)),
                         reads=[("sqb", c % 2), "onesf"], writes=[pkey])
                S.op("act", lambda e, pst=pst, rst=rst, n=n: e.activation(out=rst[:, 0:n], in_=pst, func=AF.Sqrt, bias=EPS, scale=1.0 / D),
                     reads=[pkey], writes=[rkey])
                S.op("dve", lambda e, rst=rst, n=n: e.reciprocal(out=rst[:, 0:n], in_=rst[:, 0:n]), reads=[rkey], writes=[rkey])
                for c in range(16):
                    k = key if key else ("hbuf", c // 4)
                    emit_fn(c, half, src(c), rst[:, 0:n], n, k, rkey)

        def emit_u(c, half, hb_ap, rs_ap, n, k, rkey):
            off = 0 if half == "h" else HALO + half * 512
            S.op("dve", lambda e: e.scalar_tensor_tensor(out=uT[:, c, off:off + n], in0=hb_ap, scalar=normw[:, c:c + 1], in1=rs_ap,
                                                         op0=ALU.mult, op1=ALU.mult),
                 reads=[k, rkey, "normw"], writes=[("uT", c)])

        if part == "B":
            S = _DummySched()
        rmsnorm_pass(h_d, hh_d if part == "A" else h_d, normw, emit_u)
        cut(1)
        UT_ALL = [("uT", c) for c in range(16)]

        for j in range(NT):
            for c in range(16):
                S.op("pe", lambda e, j=j, c=c: e.matmul(halo[:, j * 32:(j + 1) * 32], lhsT=uT[:, c, HALO + j * 128:HALO + (j + 1) * 128],
                                                        rhs=wdt[:, c, :], start=(c == 0), stop=(c == 15)),
                     reads=[("uT", c), "wdt"], writes=["halo"])
        HJ = ["halo"]

        def v3(t):
            return t[:].rearrange("p (j h) -> p j h", j=NT)

        S.op("dve", lambda e: e.tensor_tensor(out=v3(sm["x"]), in0=halo[:, 0:256].rearrange("p (j h) -> p j h", j=NT),
                                              in1=dtb[:].unsqueeze(1).broadcast_to([128, NT, 32]), op=ALU.add),
             reads=HJ + ["dtb"], writes=["sm_x"])
        cut(11)
        S.op("dve", lambda e: e.scalar_tensor_tensor(out=sm["e"][:], in0=sm["x"][:], scalar=-1.0, in1=sm["x"][:], op0=ALU.mult, op1=ALU.max),
             reads=["sm_x"], writes=["sm_e"])
        S.op("act", lambda e: e.activation(out=sm["e"][:], in_=sm["e"][:], func=AF.Exp, scale=-1.0), reads=["sm_e"], writes=["sm_e"])
        S.op("act", lambda e: e.activation(out=sm["e"][:], in_=sm["e"][:], func=AF.Ln, bias=1.0), reads=["sm_e"], writes=["sm_e"])
        S.op("dve", lambda e: e.scalar_tensor_tensor(out=sm["dt"][:], in0=sm["x"][:], scalar=0.0, in1=sm["e"][:], op0=ALU.max, op1=ALU.add),
             reads=["sm_x", "sm_e"], writes=["sm_dt"])
        S.op("dve", lambda e: e.tensor_tensor(out=v3(sm["adt"]), in0=v3(sm["dt"]), in1=arep[:].unsqueeze(1).broadcast_to([128, NT, 32]), op=ALU.mult),
             reads=["sm_dt", "arep"], writes=["sm_adt"])
        S.op("dve", lambda e: e.tensor_copy(out=smb[:], in_=sm["adt"][:]), reads=["sm_adt"], writes=["smb"])
        S.op("dve", lambda e: e.tensor_copy(out=sm["hi"][:], in_=smb[:]), reads=["smb"], writes=["sm_hi"])
        S.op("dve", lambda e: e.tensor_tensor(out=sm["lo"][:], in0=sm["adt"][:], in1=sm["hi"][:], op=ALU.subtract),
             reads=["sm_adt", "sm_hi"], writes=["sm_lo"])
        cut(12)
        S.op("dve", lambda e: e.tensor_copy(out=smb2[:], in_=sm["lo"][:]), reads=["sm_lo"], writes=["smb2"])
        for j in range(NT):
            S.op("pe", lambda e, j=j: e.matmul(misc[:, j * 32:(j + 1) * 32], lhsT=trib[:], rhs=smb[:, j * 32:(j + 1) * 32], start=True, stop=False),
                 reads=["trib", "smb"], writes=["misc"])
            S.op("pe", lambda e, j=j: e.matmul(misc[:, j * 32:(j + 1) * 32], lhsT=trib[:], rhs=smb2[:, j * 32:(j + 1) * 32], start=False, stop=True),
                 reads=["trib", "smb2"], writes=["misc"])
        cut(130)
        S.op("dve", lambda e: e.tensor_copy(out=sm["acs"][:], in_=misc[:, 0:256]), reads=["misc"], writes=["sm_acs"])
        cut(131)
        for j in range(NT):
            S.op("pe", lambda e, j=j: e.matmul(seg[:, j * 32:(j + 1) * 32], lhsT=onesb[:], rhs=smb[:, j * 32:(j + 1) * 32], start=True, stop=False),
                 reads=["onesb", "smb"], writes=["seg"])
            S.op("pe", lambda e, j=j: e.matmul(seg[:, j * 32:(j + 1) * 32], lhsT=onesb[:], rhs=smb2[:, j * 32:(j + 1) * 32], start=False, stop=True),
                 reads=["onesb", "smb2"], writes=["seg"])
        S.op("dve", lambda e: e.tensor_copy(out=sm["tot"][:], in_=seg[:, 0:256]), reads=["seg"], writes=["sm_tot"])
        cut(13)
        S.op("dve", lambda e: e.tensor_scalar_mul(out=sm["nacs"][:], in0=sm["acs"][:], scalar1=-1.0), reads=["sm_acs"], writes=["sm_nacs"])
        S.op("act", lambda e: e.activation(out=sm["cdec"][:], in_=sm["tot"][:], func=AF.Exp), reads=["sm_tot"], writes=["sm_cdec"])
        S.op("dve", lambda e: e.tensor_tensor(out=sm["dst"][:], in0=sm["tot"][:], in1=sm["acs"][:], op=ALU.subtract),
             reads=["sm_tot", "sm_acs"], writes=["sm_dst"])
        S.op("act", lambda e: e.activation(out=sm["dst"][:], in_=sm["dst"][:], func=AF.Exp), reads=["sm_dst"], writes=["sm_dst"])
        S.op("act", lambda e: e.activation(out=sm["eacs"][:], in_=sm["acs"][:], func=AF.Exp), reads=["sm_acs"], writes=["sm_eacs"])
        S.op("dve", lambda e: e.memset(sm["cum"][:, 0:32], 0.0), writes=["sm_cum"])
        for j in range(1, NT):
            S.op("dve", lambda e, j=j: e.tensor_tensor(out=sm["cum"][:, j * 32:(j + 1) * 32], in0=sm["cum"][:, (j - 1) * 32:j * 32],
                                                       in1=sm["tot"][:, (j - 1) * 32:j * 32], op=ALU.add),
                 reads=["sm_cum", "sm_tot"], writes=["sm_cum"])
        S.op("dve", lambda e: e.tensor_tensor(out=dtot[:], in0=sm["cum"][:, 224:256], in1=sm["tot"][:, 224:256], op=ALU.add),
             reads=["sm_cum", "sm_tot"], writes=["dtot"])
        S.op("act", lambda e: e.activation(out=dtot[:], in_=dtot[:], func=AF.Exp), reads=["dtot"], writes=["dtot"])
        S.op("dve", lambda e: e.tensor_tensor(out=sm["eglob"][:], in0=sm["acs"][:], in1=sm["cum"][:], op=ALU.add),
             reads=["sm_acs", "sm_cum"], writes=["sm_eglob"])
        S.op("act", lambda e: e.activation(out=sm["eglob"][:], in_=sm["eglob"][:], func=AF.Exp), reads=["sm_eglob"], writes=["sm_eglob"])

        cut(2)
        def proj_fm(slot, et, need_halo, evac):
            blocks = ([("h", 0, HALO)] if need_halo else []) + [(0, HALO, 512), (1, HALO + 512, 512)]
            for (half, off, n) in blocks:
                if half == "h":
                    hs = proj_fm.hslot
                    proj_fm.hslot = 8 + (hs - 8 + 1) % 7
                    pst = halo[:, hs * 32:(hs + 1) * 32]
                    pkey = "halo"
                else:
                    ai = next_acc()
                    pst = acc[ai][:]
                    pkey = ("acc", ai)
                for c in range(16):
                    S.op("pe", lambda e, c=c, pst=pst, off=off, n=n: e.matmul(pst, lhsT=Wb[slot][:, c, et * 128:(et + 1) * 128],
                                                                              rhs=uT[:, c, off:off + n], start=(c == 0), stop=(c == 15)),
                         reads=[("W", slot), ("uT", c)], writes=[pkey])
                evac(half, pst, pkey, off, n)
        proj_fm.hslot = 8

        def conv4_silu(slot, et, tile_idx, dst_ap, dst_key):
            r = raw[conv4_silu.i % 2]
            rk = ("raw", conv4_silu.i % 2)
            ct = ctmp[conv4_silu.i % 2]
            ck = ("ctmp", conv4_silu.i % 2)
            conv4_silu.i += 1

            def evac(half, pst, pkey, off, n):
                S.op("act", lambda e: e.activation(out=r[:, off:off + n], in_=pst, func=AF.Copy), reads=[pkey], writes=[rk])
            proj_fm(slot, et, True, evac)
            S.op("dve", lambda e: e.tensor_scalar_mul(out=ct[:], in0=r[:, 29:29 + T], scalar1=scw[:, tile_idx, 0:1]),
                 reads=[rk, "scw"], writes=[ck])
            for k in range(1, 4):
                S.op("dve", lambda e, k=k: e.scalar_tensor_tensor(out=ct[:], in0=r[:, 29 + k:29 + k + T], scalar=scw[:, tile_idx, k:k + 1], in1=ct[:],
                                                                  op0=ALU.mult, op1=ALU.add),
                     reads=[rk, ck, "scw"], writes=[ck])
            S.op("act", lambda e: e.activation(out=dst_ap, in_=ct[:], func=AF.Silu, bias=scb[:, tile_idx:tile_idx + 1]),
                 reads=[ck, "scb"], writes=[dst_key])
        conv4_silu.i = 0

        for g in range(8):
            hs4 = slice(g * 4, g * 4 + 4)
            slot = next_w()
            for et in range(2):
                conv4_silu(slot, et, 2 * g + et, xTg[:, et, :], ("xTg", et))
            cut(3)
            slot = next_w()
            conv4_silu(slot, 0, 16 + g, BTg[:], "BTg")
            conv4_silu(slot, 1, 24 + g, CTall[:, g, :], ("CT", g))
            slot = next_w()
            for j in range(NT):
                if j % 2 == 0:
                    ai = next_acc()
                pst = acc[ai][:, (j % 2) * 256:(j % 2 + 1) * 256]
                for c in range(16):
                    S.op("pe", lambda e, c=c, pst=pst, j=j, g=g, slot=slot: e.matmul(pst, lhsT=uT[:, c, HALO + j * 128:HALO + (j + 1) * 128], rhs=Wb[slot][:, c, :],
                                                                     start=(c == 0), stop=(c == 15)),
                         reads=[("W", slot), ("uT", c)], writes=[("acc", ai)])
                S.op("act", lambda e, pst=pst, j=j, g=g, slot=slot: e.activation(out=zsg[:, j, :], in_=pst, func=AF.Silu), reads=[("acc", ai)], writes=["zsg"])
            S.op("sp", lambda e, g=g, slot=slot: e.dma_start(out=zs_d[g], in_=zsg[:]), reads=["zsg"], writes=[("zs_d", g)], dma=("st", "zs"))
            for j in range(NT):
                for et in range(2):
                    S.op("pe", lambda e, j=j, et=et, g=g, slot=slot: e.transpose(out=trp[:, et, :], in_=xTg[:, et, j * 128:(j + 1) * 128], identity=ident[:]),
                         reads=[("xTg", et), "ident"], writes=["trp"])
                S.op("dve", lambda e, j=j, g=g, slot=slot: e.tensor_tensor(out=xdt[:, j, :].rearrange("p (h q) -> p h q", h=4),
                                                           in0=trp[:, 0:2, :].rearrange("p a (b q) -> p (a b) q", b=2),
                                                           in1=sm["dt"][:, j * 32 + g * 4:j * 32 + g * 4 + 4].unsqueeze(2).broadcast_to([128, 4, 64]),
                                                           op=ALU.mult),
                     reads=["trp", "sm_dt"], writes=["xdt"])
                S.op("dve", lambda e, j=j, g=g, slot=slot: e.tensor_tensor(out=xdec[:, j, :].rearrange("p (h q) -> p h q", h=4),
                                                           in0=xdt[:, j, :].rearrange("p (h q) -> p h q", h=4),
                                                           in1=sm["dst"][:, j * 32 + g * 4:j * 32 + g * 4 + 4].unsqueeze(2).broadcast_to([128, 4, 64]),
                                                           op=ALU.mult),
                     reads=["xdt", "sm_dst"], writes=["xdec"])
                S.op("pe", lambda e, j=j, g=g, slot=slot: e.transpose(out=trp[:, 2 + j % 2, :], in_=BTg[:, j * 128:(j + 1) * 128], identity=ident[:]),
                     reads=["BTg", "ident"], writes=["trp"])
                S.op("act", lambda e, j=j, g=g, slot=slot: e.activation(out=Btok[:, j, :], in_=trp[:, 2 + j % 2, :], func=AF.Copy),
                     reads=["trp"], writes=["Btok"])
            S.op("dve", lambda e, g=g, slot=slot: e.memset(Sloc[:], 0.0), writes=["Sloc"])
            S.op("dve", lambda e, g=g, slot=slot: e.memset(Sbf[:], 0.0), writes=["Sbf"])
            for j in range(NT):
                b = j % 2
                cols = slice(j * 32 + g * 4, j * 32 + g * 4 + 4)
                for k, nm in enumerate(("hi", "lo")):
                    S.op("pool", lambda e, k=k, nm=nm, b=b, cols=cols, g=g, slot=slot: e.tensor_tensor(
                        out=X1[0][k][:].rearrange("p (a b) -> p a b", a=4), in0=trib[:].unsqueeze(1).broadcast_to([128, 4, 128]),
                        in1=sm[nm][:, cols].unsqueeze(2).broadcast_to([128, 4, 128]), op=ALU.mult),
                        reads=["trib", "sm_" + nm], writes=[("X1", 0, k)])
                S.op("pe", lambda e, b=b, g=g, slot=slot: e.matmul(seg[:], lhsT=onesb[:], rhs=X1[0][0][:], start=True, stop=False),
                     reads=["onesb", ("X1", 0, 0)], writes=["seg"])
                S.op("pe", lambda e, b=b, g=g, slot=slot: e.matmul(seg[:], lhsT=onesb[:], rhs=X1[0][1][:], start=False, stop=False),
                     reads=["onesb", ("X1", 0, 1)], writes=["seg"])
                S.op("pe", lambda e, g=g, slot=slot: e.matmul(seg[:], lhsT=ident[:], rhs=nmask[:], start=False, stop=True),
                     reads=["ident", "nmask"], writes=["seg"])
                for hh in range(4):
                    col = j * 32 + g * 4 + hh
                    S.op("act", lambda e, hh=hh, b=b, col=col, g=g, slot=slot: e.activation(out=dec[b][:, hh * 128:(hh + 1) * 128], in_=seg[:, hh * 128:(hh + 1) * 128], func=AF.Exp,
                                                                            bias=sm["nacs"][:, col:col + 1]),
                         reads=["seg", "sm_nacs"], writes=[("dec", b)])
                S.op("pe", lambda e, j=j, g=g, slot=slot: e.matmul(misc[:, 0:128], lhsT=BTg[:, j * 128:(j + 1) * 128], rhs=CTall[:, g, j * 128:(j + 1) * 128],
                                                   start=True, stop=True),
                     reads=["BTg", ("CT", g)], writes=["misc"])
                S.op("dve", lambda e, b=b, g=g, slot=slot: e.tensor_tensor(out=MT[b][:].rearrange("p (a b) -> p a b", a=4), in0=dec[b][:].rearrange("p (a b) -> p a b", a=4), in1=misc[:, 0:128].unsqueeze(1).broadcast_to([128, 4, 128]),
                                                           op=ALU.mult),
                     reads=[("dec", b), "misc"], writes=[("MT", b)])
                S.op("pe", lambda e, j=j, g=g, slot=slot: e.matmul(yg[:, 256:512], lhsT=CTall[:, g, j * 128:(j + 1) * 128], rhs=Sbf[:], start=True, stop=True),
                     reads=[("CT", g), "Sbf"], writes=["yg"])
                for hh in range(4):
                    et, h2 = hh // 2, hh % 2
                    S.op("pe", lambda e, hh=hh, b=b, j=j, g=g, slot=slot: e.matmul(yg[:, hh * 64:(hh + 1) * 64], lhsT=MT[b][:, hh * 128:(hh + 1) * 128], rhs=xdt[:, j, hh * 64:(hh + 1) * 64],
                                                                   start=True, stop=False),
                         reads=[("MT", b), "xdt"], writes=["yg"])
                    S.op("pe", lambda e, hh=hh, et=et, h2=h2, j=j, g=g, slot=slot: e.matmul(yg[:, hh * 64:(hh + 1) * 64], lhsT=xTg[:, et, j * 128:(j + 1) * 128],
                                                                            rhs=diagD[:, 2 * g + et, h2 * 64:(h2 + 1) * 64], start=False, stop=True),
                         reads=[("xTg", et), "diagD"], writes=["yg"])
                S.op("dve", lambda e, b=b, cols=cols, g=g, slot=slot: e.tensor_tensor(out=t1[b][:].rearrange("p (h q) -> p h q", h=4),
                                                                      in0=yg[:, 256:512].rearrange("p (h q) -> p h q", h=4),
                                                                      in1=sm["eacs"][:, cols].unsqueeze(2).broadcast_to([128, 4, 64]), op=ALU.mult),
                     reads=["yg", "sm_eacs"], writes=[("t1", b)])
                S.op("dve", lambda e, b=b, j=j, g=g, slot=slot: e.tensor_tensor(out=ylb[:, j, :], in0=yg[:, 0:256], in1=t1[b][:], op=ALU.add),
                     reads=["yg", ("t1", b)], writes=["ylb"])
                S.op("pe", lambda e, j=j, g=g, slot=slot: e.matmul(misc[:, 128:384], lhsT=Btok[:, j, :], rhs=xdec[:, j, :], start=True, stop=True),
                     reads=["Btok", "xdec"], writes=["misc"])
                S.op("dve", lambda e, cols=cols, g=g, slot=slot: e.tensor_tensor(out=Sloc[:].rearrange("p (h q) -> p h q", h=4),
                                                                 in0=Sloc[:].rearrange("p (h q) -> p h q", h=4),
                                                                 in1=sm["cdec"][:, cols].unsqueeze(2).broadcast_to([128, 4, 64]), op=ALU.mult),
                     reads=["Sloc", "sm_cdec"], writes=["Sloc"])
                S.op("dve", lambda e, g=g, slot=slot: e.tensor_tensor(out=Sloc[:], in0=misc[:, 128:384], in1=Sloc[:], op=ALU.add), reads=["misc", "Sloc"], writes=["Sloc"])
                S.op("act", lambda e, g=g, slot=slot: e.activation(out=Sbf[:], in_=Sloc[:], func=AF.Copy), reads=["Sloc"], writes=["Sbf"])
            S.op("sp", lambda e, g=g, slot=slot: e.dma_start(out=yloc_d[g], in_=ylb[:]), reads=["ylb"], writes=[("yloc_d", g)], dma=("st", "yl"))
            S.op("sp", lambda e, g=g, slot=slot: e.dma_start(out=fsend_d[:, g * 256:(g + 1) * 256], in_=Sloc[:]), reads=["Sloc"], writes=["fsend"], dma=("st", "fs"))
            if g == 0:
                for nm in ("dt", "adt", "acs", "tot", "dst", "eacs", "cdec", "nacs", "hi", "lo"):
                    dbg("sm_" + nm, sm[nm][:], "sm_" + nm)
                dbg("xTg", xTg[:], ("xTg", 0), BF16)
                dbg("BTg", BTg[:], "BTg", BF16)
                dbg("CT0", CTall[:, 0, :], ("CT", 0), BF16)
                dbg("zsg", zsg[:], "zsg", BF16)
                dbg("xdt", xdt, "xdt", BF16)
                dbg("Btok", Btok[:], "Btok", BF16)
                dbg("dec1", dec[1][:], ("dec", 1), BF16)
                dbg("MT1", MT[1][:], ("MT", 1), BF16)
                dbg("Sloc", Sloc[:], "Sloc")
                dbg("ylb", ylb[:], "ylb")
                dbg("nmask", nmask[:], "nmask", BF16)
                dbg("trib", trib[:], "trib", BF16)
                dbg("uT", uT[:], ("uT", 0), BF16)
            cut(4)
        S.op("sp", lambda e, g=g, slot=slot: e.dma_start(out=fsend_d[:, 2048:2080], in_=dtot[:]), reads=["dtot"], writes=["fsend"], dma=("st", "fs"))
        if DEBUG and STOP == 5:
            d_ = nc.dram_tensor("dbg_yloc", [8, 128, NT, 256], F32, kind="ExternalOutput").ap()
            o_ = S.op("sp", lambda e, g=g, slot=slot: e.dma_start(out=d_, in_=yloc_d), reads=[("yloc_d", g_) for g_ in range(8)], dma=("st", "dbg_yloc"))
            S.must_finish(o_)
        cut(5)
        def out_proj(src_d, n_units):
            for dp in range(n_units):
                slot = next_w()
                for dti in range(2):
                    dch = dp * 2 + dti
                    for half in range(2):
                        ai = next_acc()
                        for ec in range(16):
                            S.op("pe", lambda e, ec=ec, ai=ai, half=half, dti=dti, slot=slot: e.matmul(
                                acc[ai][:], lhsT=Wb[slot][:, ec, dti * 128:(dti + 1) * 128], rhs=mixT[:, ec, half * 512:(half + 1) * 512],
                                start=(ec == 0), stop=(ec == 15)),
                                reads=[("W", slot), "mixT"], writes=[("acc", ai)])
                        hb = out_proj.i % 2
                        out_proj.i += 1
                        S.op("sp", lambda e, hb=hb, dch=dch, half=half: e.dma_start(out=hb2[hb][:], in_=src_d[:, dch, half * 512:(half + 1) * 512]),
                             reads=[("hd", dch, half)], writes=[("hb2", hb)], dma=("ld", "hb2_%d" % hb))
                        S.op("dve", lambda e, hb=hb, ai=ai: e.tensor_tensor(out=hb2[hb][:], in0=acc[ai][:], in1=hb2[hb][:], op=ALU.add),
                             reads=[("acc", ai), ("hb2", hb)], writes=[("hb2", hb)])
                        o = S.op("sp", lambda e, hb=hb, dch=dch, half=half: e.dma_start(out=hout_d[:, dch, half * 512:(half + 1) * 512], in_=hb2[hb][:]),
                                 reads=[("hb2", hb)], writes=[("hd", dch, half)], dma=("st", "ho%d" % hb))
                        S.must_finish(o)
        out_proj.i = 0

        for i in range(8):
            slot_a = next_w()
            slot_v = next_w(prefetch=False)
            for et in range(2):
                ct = 2 * i + et
                b = ct % 2
                def evac_a(half, pst, pkey, off, n, b=b):
                    S.op("act", lambda e: e.activation(out=sig[b][:, off:off + n], in_=pst, func=AF.Sigmoid), reads=[pkey], writes=[("raw", b)])
                proj_fm(slot_a, et, True, evac_a)

                def evac_v(half, pst, pkey, off, n, b=b):
                    S.op("dve", lambda e: e.tensor_tensor(out=glu[b][:, off:off + n], in0=pst, in1=sig[b][:, off:off + n], op=ALU.mult),
                         reads=[pkey, ("raw", b)], writes=[("glu", b)])
                proj_fm(slot_v, et, True, evac_v)
                S.op("pool", lambda e, b=b, ct=ct: e.tensor_tensor(out=dg[b], in0=identf[:].unsqueeze(1).broadcast_to([128, 31, 128]),
                                                                   in1=ccw[:, ct, :].unsqueeze(2).broadcast_to([128, 31, 128]), op=ALU.mult),
                     reads=["identf", "ccw"], writes=[("dg", 0), "ylb"])
                for half in range(2):
                    ai = next_acc()
                    for k in range(31):
                        S.op("pe", lambda e, k=k, ai=ai, b=b, half=half: e.matmul(acc[ai][:], lhsT=dg[b][:, k, :],
                                                                                  rhs=glu[b][:, 2 + half * 512 + k:2 + half * 512 + k + 512],
                                                                                  start=(k == 0), stop=(k == 30)),
                             reads=[("dg", 0), ("glu", b)], writes=[("acc", ai)])
                    S.op("act", lambda e, ai=ai, ct=ct, half=half: e.activation(out=mixT[:, ct, half * 512:(half + 1) * 512], in_=acc[ai][:],
                                                                                func=AF.Identity, bias=ccb[:, ct:ct + 1]),
                         reads=[("acc", ai), "ccb"], writes=[("mixT", ct, half)])
        if DEBUG and STOP == 6:
            S.op("dve", lambda e: e.memset(ss[0][:], 0.0), reads=[("mixT", ct_, h_) for ct_ in range(16) for h_ in range(2)], writes=["mixT"])
            dbg("ccraw", mixT[:], "mixT", BF16)
        cut(6)
        for half in range(2):
            a1 = next_acc()
            a2 = next_acc()
            hsl = slice(half * 512, (half + 1) * 512)
            for ct in range(16):
                S.op("pe", lambda e, ct=ct, a1=a1, hsl=hsl: e.matmul(acc[a1][:], lhsT=onesb[:], rhs=mixT[:, ct, hsl], start=(ct == 0), stop=(ct == 15)),
                     reads=[("mixT", ct, half), "onesb"], writes=[("acc", a1)])
                S.op("act", lambda e, ct=ct, hsl=hsl: e.activation(out=sq2[ct % 2], in_=mixT[:, ct, hsl], func=AF.Square),
                     reads=[("mixT", ct, half)], writes=[("sq2", ct % 2)])
                S.op("pe", lambda e, ct=ct, a2=a2: e.matmul(acc[a2][:], lhsT=onesb[:], rhs=sq2[ct % 2], start=(ct == 0), stop=(ct == 15)),
                     reads=[("sq2", ct % 2), "onesb"], writes=[("acc", a2)])
            S.op("act", lambda e, a1=a1: e.activation(out=mub, in_=acc[a1][:], func=AF.Copy, scale=1.0 / D), reads=[("acc", a1)], writes=["mub"])
            S.op("dve", lambda e: e.tensor_tensor(out=nmr, in0=mub, in1=mub, op=ALU.mult), reads=["mub"], writes=["nmr"])
            S.op("dve", lambda e, a2=a2: e.scalar_tensor_tensor(out=rsb, in0=acc[a2][:], scalar=1.0 / D, in1=nmr, op0=ALU.mult, op1=ALU.subtract),
                 reads=[("acc", a2), "nmr"], writes=["rsb"])
            S.op("act", lambda e: e.activation(out=rsb, in_=rsb, func=AF.Sqrt, bias=EPS), reads=["rsb"], writes=["rsb"])
            S.op("dve", lambda e: e.reciprocal(out=rsb, in_=rsb), reads=["rsb"], writes=["rsb"])
            S.op("dve", lambda e: e.scalar_tensor_tensor(out=nmr, in0=mub, scalar=-1.0, in1=rsb, op0=ALU.mult, op1=ALU.mult),
                 reads=["mub", "rsb"], writes=["nmr"])
            for ct in range(16):
                b = ct % 2
                S.op("dve", lambda e, ct=ct, b=b, hsl=hsl: e.tensor_tensor(out=lt[b][:], in0=mixT[:, ct, hsl], in1=rsb, op=ALU.mult),
                     reads=[("mixT", ct, half), "rsb"], writes=[("lt", b)])
                S.op("dve", lambda e, b=b: e.tensor_tensor(out=lt[b][:], in0=lt[b][:], in1=nmr, op=ALU.add),
                     reads=[("lt", b), "nmr"], writes=[("lt", b)])
                S.op("act", lambda e, ct=ct, b=b, hsl=hsl: e.activation(out=mixT[:, ct, hsl], in_=lt[b][:], func=AF.Silu,
                                                                        bias=lnb[:, ct:ct + 1], scale=lnw[:, ct:ct + 1]),
                     reads=[("lt", b), "lnw", "lnb"], writes=[("mixT", ct, half)])
        for i in range(8):
            slot = next_w()
            for et in range(2):
                ct = 2 * i + et

                def evac_g(half, pst, pkey, off, n, ct=ct):
                    b = evac_g.i % 2
                    evac_g.i += 1
                    hsl = slice(half * 512, (half + 1) * 512)
                    S.op("act", lambda e: e.activation(out=cgs[b][:], in_=pst, func=AF.Silu), reads=[pkey], writes=[("cgs", b)])
                    S.op("dve", lambda e: e.tensor_tensor(out=mixT[:, ct, hsl], in0=mixT[:, ct, hsl], in1=cgs[b][:], op=ALU.mult),
                         reads=[("cgs", b), ("mixT", ct, half)], writes=[("mixT", ct, half)])
                evac_g.i = 0
                proj_fm(slot, et, False, evac_g)
        MIX_ALL = [("mixT", ct, half) for ct in range(16) for half in range(2)]
        S.op("dve", lambda e: e.memset(ss[0][:], 0.0), reads=MIX_ALL, writes=["mixT"])
        dbg("mixconv", mixT[:], "mixT", BF16)
        cut(65)
        out_proj(h_d, 8)
        cut(7)
        if part == "A":
            o_ = S.op("sp", lambda e: e.dma_start(out=ct_d, in_=CTall[:]), reads=[("CT", g_) for g_ in range(8)], dma=("st", "ctd"))
            S.must_finish(o_)
            o_ = S.op("sp", lambda e: e.dma_start(out=eg_d, in_=sm["eglob"][:]), reads=["sm_eglob"], dma=("st", "egd"))
            S.must_finish(o_)
            for o_ in list(S_real.ops["sp"]):
                if o_.dma_key is not None and o_.dma_key[0] == "st" and o_.dma_key[1] in ("zs", "yl", "fs"):
                    S.must_finish(o_)
            raise _Stop()
        S = S_real
        wstate["issued"] = wpos[0]
        out_proj.i = 0
        accr[0] = 0
        S.op("sp", lambda e: e.dma_start(out=CTall[:], in_=ct_d), writes=[("CT", g_) for g_ in range(8)], dma=("ld", "ctd"))
        S.op("sp", lambda e: e.dma_start(out=sm["eglob"][:], in_=eg_d), writes=["sm_eglob"], dma=("ld", "egd"))

        fr = frecv_d.rearrange("(r p) f -> p r f", p=128)
        S.op("sp", lambda e: e.dma_start(out=Dg[:, 0:ncores, :], in_=fr[:, :, 2048:2080]), reads=["frecv"], writes=["Dg"], dma=("ld", "Dg"))
        for g in range(8):
            o = S.op("sp", lambda e, g=g: e.dma_start(out=Fg[:, 0:ncores, :], in_=fr[:, :, g * 256:(g + 1) * 256]), reads=["frecv"], writes=["Fg"], dma=("ld", "Fg"))
            S.op("sp", lambda e, g=g: e.dma_start(out=ylb[:], in_=yloc_d[g]), reads=[("yloc_d", g)], writes=["ylb"], dma=("ld", "ylb"))
            S.op("sp", lambda e, g=g: e.dma_start(out=zl[:], in_=zs_d[g]), reads=[("zs_d", g)], writes=["zsg"], dma=("ld", "zl"))
            S.op("sp", lambda e, g=g: e.dma_start(out=snw[:], in_=snw_d[:, g * 256:(g + 1) * 256].partition_broadcast(128)), writes=["snw"], dma=("ld", "snw"))
            S.op("dve", lambda e: e.memset(Sin[:], 0.0), writes=["Sloc"])
            for r in range(ncores):
                S.op("dve", lambda e, r=r: e.tensor_scalar(out=ar[:], in0=Dg[:, r, :], scalar1=msk[:, r:r + 1], scalar2=msk[:, 8 + r:9 + r],
                                                           op0=ALU.mult, op1=ALU.add),
                     reads=["Dg", "msk"], writes=["ar"])
                S.op("dve", lambda e, g=g: e.tensor_tensor(out=Sin[:].rearrange("p (h q) -> p h q", h=4), in0=Sin[:].rearrange("p (h q) -> p h q", h=4),
                                                           in1=ar[:, g * 4:g * 4 + 4].unsqueeze(2).broadcast_to([128, 4, 64]), op=ALU.mult),
                     reads=["Sloc", "ar"], writes=["Sloc"])
                S.op("dve", lambda e, r=r: e.scalar_tensor_tensor(out=Sin[:], in0=Fg[:, r, :], scalar=msk[:, r:r + 1], in1=Sin[:], op0=ALU.mult, op1=ALU.add),
                     reads=["Fg", "Sloc", "msk"], writes=["Sloc"])
            S.op("act", lambda e: e.activation(out=Sinb[:], in_=Sin[:], func=AF.Copy), reads=["Sloc"], writes=["Sbf"])
            for j in range(NT):
                b = j % 2
                cols = slice(j * 32 + g * 4, j * 32 + g * 4 + 4)
                S.op("pe", lambda e, j=j, g=g: e.matmul(yg[:, 256:512], lhsT=CTall[:, g, j * 128:(j + 1) * 128], rhs=Sinb[:], start=True, stop=True),
                     reads=[("CT", g), "Sbf"], writes=["yg"])
                S.op("dve", lambda e, b=b, cols=cols: e.tensor_tensor(out=t1[b][:].rearrange("p (h q) -> p h q", h=4),
                                                                      in0=yg[:, 256:512].rearrange("p (h q) -> p h q", h=4),
                                                                      in1=sm["eglob"][:, cols].unsqueeze(2).broadcast_to([128, 4, 64]), op=ALU.mult),
                     reads=["yg", "sm_eglob"], writes=[("t1", b)])
                S.op("dve", lambda e, b=b, j=j: e.tensor_tensor(out=t1[b][:], in0=t1[b][:], in1=ylb[:, j, :], op=ALU.add),
                     reads=[("t1", b), "ylb"], writes=[("t1", b)])
                S.op("dve", lambda e, b=b, j=j: e.tensor_tensor(out=yz[b], in0=t1[b][:], in1=zl[:, j, :], op=ALU.mult),
                     reads=[("t1", b), "zsg"], writes=[("yz", b)])
                S.op("act", lambda e, b=b: e.activation(out=ysq, in_=yz[b], func=AF.Square, accum_out=ss[b][:]),
                     reads=[("yz", b)], writes=["ysq", ("ss", b)])
                S.op("act", lambda e, b=b: e.activation(out=ss[b][:], in_=ss[b][:], func=AF.Sqrt, bias=EPS, scale=1.0 / 256), reads=[("ss", b)], writes=[("ss", b)])
                S.op("dve", lambda e, b=b: e.reciprocal(out=ss[b][:], in_=ss[b][:]), reads=[("ss", b)], writes=[("ss", b)])
                S.op("dve", lambda e, b=b, g=g: e.scalar_tensor_tensor(out=yn[b][:], in0=yz[b], scalar=ss[b][:, 0:1], in1=snw[:],
                                                                       op0=ALU.mult, op1=ALU.mult),
                     reads=[("yz", b), ("ss", b), "snw"], writes=[("yn", b)])
                for et in range(2):
                    sl = 4 + 2 * b + et
                    S.op("pe", lambda e, b=b, et=et, sl=sl: e.transpose(out=trp[:, sl, :], in_=yn[b][:, et * 128:(et + 1) * 128], identity=ident[:]),
                         reads=[("yn", b), "ident"], writes=["trp"])
                    S.op("act", lambda e, sl=sl, g=g, et=et, j=j: e.activation(out=mixT[:, 2 * g + et, j * 128:(j + 1) * 128], in_=trp[:, sl, :], func=AF.Copy),
                         reads=["trp"], writes=["mixT"])
            if g == 0:
                dbg("p3_Sin", Sin[:], "Sloc")
                dbg("p3_t1", t1[1][:], ("t1", 1))
                dbg("p3_yz", yz[1], ("yz", 1))
                dbg("p3_ss", ss[1][:], ("ss", 1))
                dbg("p3_yn", yn[1][:], ("yn", 1), BF16)
                dbg("p3_eglob", sm["eglob"][:], "sm_eglob")
                dbg("p3_Fg", Fg[:, 0:ncores, :], "Fg")
                dbg("p3_Dg", Dg[:, 0:ncores, :], "Dg")
                dbg("p3_ar", ar[:], "ar")
                dbg("p3_ylb", ylb[:], "ylb")
                dbg("p3_zl", zl[:], "zsg", BF16)
                dbg("p3_snw", snw[:], "snw")
                dbg("p3_mix", mixT[:, 0:2, :], "mixT", BF16)
            cut(74)
        if DEBUG:
            d_ = nc.dram_tensor("dbg_yloc", [8, 128, NT, 256], F32, kind="ExternalOutput").ap()
            o_ = S.op("sp", lambda e: e.dma_start(out=d_, in_=yloc_d), reads=[("yloc_d", g_) for g_ in range(8)], dma=("st", "dbg_yloc"))
            S.must_finish(o_)
            d2_ = nc.dram_tensor("dbg_zs", [8, 128, NT, 256], BF16, kind="ExternalOutput").ap()
            o2_ = S.op("sp", lambda e: e.dma_start(out=d2_, in_=zs_d), reads=[("zs_d", g_) for g_ in range(8)], dma=("st", "dbg_zs"))
            S.must_finish(o2_)
        dbg("mixall", mixT[:], "mixT", BF16)
        cut(75)
        out_proj(h_d, 8)
        cut(8)

        if last:
            def emit_o(c, half, hb_ap, rs_ap, n, k, rkey):
                hb = emit_o.i % 2
                emit_o.i += 1
                S.op("dve", lambda e: e.scalar_tensor_tensor(out=hb2[hb][:], in0=hb_ap, scalar=fnw[:, c:c + 1], in1=rs_ap, op0=ALU.mult, op1=ALU.mult),
                     reads=[k, rkey, "fnw"], writes=[("hb2", hb)])
                o = S.op("sp", lambda e: e.dma_start(out=out_d[:, c, half * 512:(half + 1) * 512], in_=hb2[hb][:]),
                         reads=[("hb2", hb)], dma=("st", "ho%d" % hb))
                S.must_finish(o)
            emit_o.i = 0

            HD_ALL = [("hd", dch, half) for dch in range(16) for half in range(2)]
            S.op("sp", lambda e: e.dma_start(out=hhb[:], in_=hout_d[:, :, 0:HALO]), reads=HD_ALL, writes=["hhb", "hout_all"], dma=("ld", "hhb"))

            def rms_final():
                for half in range(2):
                    off = half * 512
                    for q in range(4):
                        S.op("sp", lambda e, q=q, off=off: e.dma_start(out=hbuf[:, 4 * q:4 * q + 4, :], in_=hout_d[:, 4 * q:4 * q + 4, off:off + 512]),
                             reads=["hout_all"], writes=[("hbuf", q)], dma=("ld", "hbuf%d" % q))
                    ai = next_acc()
                    for c in range(16):
                        sb = sqb[c % 2]
                        S.op("act", lambda e, c=c, sb=sb: e.activation(out=sb[:], in_=hbuf[:, c, :], func=AF.Square),
                             reads=[("hbuf", c // 4)], writes=[("sqb", c % 2)])
                        S.op("pe", lambda e, c=c, sb=sb, ai=ai: e.matmul(acc[ai][:], lhsT=onesf[:], rhs=sb[:], start=(c == 0), stop=(c == 15)),
                             reads=[("sqb", c % 2), "onesf"], writes=[("acc", ai)])
                    S.op("act", lambda e, ai=ai: e.activation(out=rs[:], in_=acc[ai][:], func=AF.Sqrt, bias=EPS, scale=1.0 / D), reads=[("acc", ai)], writes=["rs"])
                    S.op("dve", lambda e: e.reciprocal(out=rs[:], in_=rs[:]), reads=["rs"], writes=["rs"])
                    for c in range(16):
                        emit_o(c, half, hbuf[:, c, :], rs[:], 512, ("hbuf", c // 4), "rs")
            rms_final()
        pass
      except _Stop:
        pass
      S_holder[0].emit()
    return nc


def _fm(v, nch):
    return np.ascontiguousarray(v.reshape(nch, 128).T)


def _pack_layer(inp, l):
    w_in = inp["w_in"][l]
    cols = []
    for g in range(8):
        cols.append(np.arange(2048 + 256 * g, 2048 + 256 * (g + 1)))
        cols.append(np.concatenate([np.arange(4096 + 128 * g, 4096 + 128 * (g + 1)), np.arange(5120 + 128 * g, 5120 + 128 * (g + 1))]))
        cols.append(np.arange(256 * g, 256 * (g + 1)))
    for i in range(8):
        cols.append(np.arange(8224 + 256 * i, 8224 + 256 * (i + 1)))
        cols.append(np.arange(6176 + 256 * i, 6176 + 256 * (i + 1)))
    for i in range(8):
        cols.append(np.arange(10272 + 256 * i, 10272 + 256 * (i + 1)))
    win = np.empty((48, 128, 16, 256), np.float32)
    w3 = w_in.reshape(16, 128, -1)
    for u, cc in enumerate(cols):
        win[u] = w3[:, :, cc].transpose(1, 0, 2)
    wdt = np.ascontiguousarray(w3[:, :, 6144:6176].transpose(1, 0, 2))
    w_out = inp["w_out"][l]
    wo = w_out.reshape(2, 16, 128, 8, 256)
    wout = np.ascontiguousarray(wo.transpose(0, 3, 2, 1, 4)).reshape(16, 128, 16, 256)
    scw = np.ascontiguousarray(inp["ssm_conv_w"][l].reshape(4, 32, 128).transpose(2, 1, 0))
    ccw = np.ascontiguousarray(inp["conf_conv_w"][l].reshape(31, 16, 128).transpose(2, 1, 0))
    dsk = np.ascontiguousarray(np.repeat(inp["d_skip"][l], 64).reshape(16, 128).T)
    return {
        "win": win, "wdt": wdt, "wout": wout,
        "normw": _fm(inp["norm_w"][l], 16), "scw": scw, "scb": _fm(inp["ssm_conv_b"][l], 32),
        "dtb": np.ascontiguousarray(inp["dt_bias"][l][None, :]), "alog": np.ascontiguousarray(inp["a_log"][l][None, :]),
        "dsk": dsk, "snw": np.ascontiguousarray(inp["ssm_norm_w"][l][None, :]),
        "ccw": ccw, "ccb": _fm(inp["conf_conv_b"][l], 16), "lnw": _fm(inp["ln_w"][l], 16), "lnb": _fm(inp["ln_b"][l], 16),
        "fnw": _fm(inp["final_norm_w"], 16),
    }


_NC_CACHE = {}


def _get_nc(ncores, last, part):
    key = (ncores, last, part)
    if key not in _NC_CACHE:
        _NC_CACHE[key] = build_layer(ncores, last, part)
    return _NC_CACHE[key]


_SMALL = ("normw", "scw", "scb", "dtb", "alog", "dsk", "snw", "ccw", "ccb", "lnw", "lnb", "fnw")


def _run(inp, ncores=NCORES, depth=DEPTH, final=True):
    inp = {k: np.asarray(v, dtype=np.float32) for k, v in inp.items()}
    x = inp["x"][0]
    ntok = ncores * T
    xT = np.ascontiguousarray(x[:ntok].T)
    hs = [np.ascontiguousarray(xT[:, r * T:(r + 1) * T].reshape(16, 128, T).transpose(1, 0, 2)) for r in range(ncores)]
    msks = []
    for r in range(ncores):
        m = np.zeros((128, 16), np.float32)
        for j in range(8):
            m[:, j] = 1.0 if j < r else 0.0
            m[:, 8 + j] = 1.0 - m[:, j]
        msks.append(m)
    cores = list(range(ncores))
    res = None
    for l in range(depth):
        last = final and (l == depth - 1)
        pk = _pack_layer(inp, l)
        small = {k: pk[k] for k in _SMALL}
        in_maps = []
        for r in range(ncores):
            m = dict(small)
            m["win"] = pk["win"]
            m["wdt"] = pk["wdt"]
            m["wout"] = pk["wout"][8:16]
            m["h"] = hs[r]
            m["hh"] = np.ascontiguousarray(hs[r - 1][:, :, T - HALO:]) if r > 0 else np.zeros((128, 16, HALO), np.float32)
            m["msk"] = msks[r]
            in_maps.append(m)
        ra = run_bass_kernel_spmd(_get_nc(ncores, False, "A"), in_maps, core_ids=cores).results
        del in_maps
        frecv = np.ascontiguousarray(np.concatenate([np.asarray(ra[r]["fsend"]) for r in range(ncores)], axis=0))
        in_maps = []
        for r in range(ncores):
            m = dict(small)
            m["wout"] = pk["wout"][0:8]
            m["hpart"] = np.asarray(ra[r]["hout"])
            m["yloc"] = np.asarray(ra[r]["yloc"])
            m["zsd"] = np.asarray(ra[r]["zsd"])
            m["ctd"] = np.asarray(ra[r]["ctd"])
            m["egd"] = np.asarray(ra[r]["egd"])
            m["frecv"] = frecv
            m["msk"] = msks[r]
            in_maps.append(m)
        res = run_bass_kernel_spmd(_get_nc(ncores, last, "B"), in_maps, core_ids=cores).results
        del in_maps, ra
        hs = [np.asarray(res[r]["hout"]) for r in range(ncores)]
    key = "out" if final else "hout"
    outT = np.concatenate([np.asarray(res[r][key]).transpose(1, 0, 2).reshape(D, T) for r in range(ncores)], axis=1)
    return np.ascontiguousarray(outT.T)[None].astype(np.float32)


def kernel(**inputs):
    return _run(inputs)
```
